# Optimizing a Trainium2 kernel written in Bass

```python
import jax, jax.numpy as jnp
from jax import lax
import numpy as np

D_MODEL = 1024
BATCH = 4
SEQ = 4096
DEPTH = 2

N_MIXERS = 2
N_MLSTM_LAYERS = (DEPTH + 1) // 2
N_RWKV_LAYERS = DEPTH // 2
DN_ALPHA = (2.0 * DEPTH) ** 0.25
DN_BETA = (8.0 * DEPTH) ** -0.25
LN_EPS = 1e-5
D_FF = 4 * D_MODEL

M_HEADS = 4
M_DV = D_MODEL // M_HEADS
M_DK = M_DV // 2
M_CHUNK = 64
M_CONV = 4
M_GATE_CAP = 15.0
M_QK = 2 * M_HEADS * M_DK
M_PROJ = M_QK + 2 * M_HEADS * M_DV + 2 * M_HEADS

R_N = 64
R_HEADS = D_MODEL // R_N
R_LW = D_MODEL // 16
R_LA = D_MODEL // 16
R_LG = D_MODEL // 8
R_PROJ = 3 * D_MODEL + R_LW + R_LA + R_LG
R_GN_EPS = 64e-5

kernel_name = 'hybrid_mlstm_rwkv7_deepnorm'


def layer_norm(x, g, b):
    xf = x.astype(jnp.float32)
    mu = jnp.mean(xf, -1, keepdims=True)
    var = jnp.mean(jnp.square(xf - mu), -1, keepdims=True)
    return ((xf - mu) * lax.rsqrt(var + LN_EPS) * g + b).astype(x.dtype)


def causal_depthwise_conv(x, w, b):
    k = w.shape[0]
    y = lax.conv_general_dilated(x, w[:, None, :].astype(x.dtype), window_strides=(1,),
                                 padding=[(k - 1, 0)], dimension_numbers=('NWC', 'WIO', 'NWC'),
                                 feature_group_count=x.shape[-1])
    return y + b


def token_shift(x):
    return jnp.pad(x, ((0, 0), (1, 0), (0, 0)))[:, :-1]


def soft_cap(x, cap):
    return cap * jnp.tanh(x / cap)


def mlstm_chunkwise(q, k, v, log_i, log_f):
    bsz, seq = q.shape[0], q.shape[1]
    nc = seq // M_CHUNK

    def chunks(t):
        return t.reshape(bsz, nc, M_CHUNK, M_HEADS, -1).transpose(1, 0, 3, 2, 4)

    def gchunks(t):
        return t.reshape(bsz, nc, M_CHUNK, M_HEADS).transpose(1, 0, 3, 2)

    causal = jnp.tril(jnp.ones((M_CHUNK, M_CHUNK), dtype=bool))

    def step(carry, inp):
        c_prev, n_prev, m_prev = carry
        qc, kc, vc, ic, fc = inp
        bcum = jnp.cumsum(fc, axis=-1)
        log_d = bcum[..., :, None] - bcum[..., None, :] + ic[..., None, :]
        log_d = jnp.where(causal, log_d, -jnp.inf)
        log_inter = bcum + m_prev[..., None]
        m_row = jnp.maximum(jnp.max(log_d, -1), log_inter)
        d_mat = jnp.exp(log_d - m_row[..., None])
        inter = jnp.exp(log_inter - m_row)
        s = jnp.einsum('bhld,bhsd->bhls', qc, kc) * d_mat
        num = jnp.einsum('bhls,bhse->bhle', s, vc) + inter[..., None] * jnp.einsum('bhld,bhde->bhle', qc, c_prev)
        den = jnp.sum(s, -1) + inter * jnp.einsum('bhld,bhd->bhl', qc, n_prev)
        hc = num / jnp.maximum(jnp.abs(den), jnp.exp(-m_row))[..., None]
        b_last = bcum[..., -1]
        log_w = b_last[..., None] - bcum + ic
        m_new = jnp.maximum(b_last + m_prev, jnp.max(log_w, -1))
        decay = jnp.exp(b_last + m_prev - m_new)
        wk = jnp.exp(log_w - m_new[..., None])
        c_new = decay[..., None, None] * c_prev + jnp.einsum('bhl,bhld,bhle->bhde', wk, kc, vc)
        n_new = decay[..., None] * n_prev + jnp.einsum('bhl,bhld->bhd', wk, kc)
        return (c_new, n_new, m_new), hc

    init = (jnp.zeros((bsz, M_HEADS, M_DK, M_DV), jnp.float32),
            jnp.zeros((bsz, M_HEADS, M_DK), jnp.float32),
            jnp.zeros((bsz, M_HEADS), jnp.float32))
    _, h = lax.scan(step, init, (chunks(q), chunks(k), chunks(v), gchunks(log_i), gchunks(log_f)))
    return h.transpose(1, 0, 3, 2, 4).reshape(bsz, seq, M_HEADS, M_DV)


def mlstm_mixer(x, w_in, b_i, b_f, conv_w, conv_b, norm_g, w_out):
    bsz, seq, _ = x.shape
    f32 = jnp.float32
    proj = x @ w_in
    qk = jax.nn.silu(causal_depthwise_conv(proj[..., :M_QK], conv_w, conv_b)).astype(f32)
    q = qk[..., :M_QK // 2].reshape(bsz, seq, M_HEADS, M_DK) * (M_DK ** -0.5)
    k = qk[..., M_QK // 2:].reshape(bsz, seq, M_HEADS, M_DK)
    o0 = M_QK
    o1 = o0 + M_HEADS * M_DV
    o2 = o1 + M_HEADS * M_DV
    v = proj[..., o0:o1].astype(f32).reshape(bsz, seq, M_HEADS, M_DV)
    o_gate = jax.nn.sigmoid(proj[..., o1:o2].astype(f32))
    gates = proj[..., o2:].astype(f32)
    log_i = soft_cap(gates[..., :M_HEADS] + b_i, M_GATE_CAP)
    log_f = jax.nn.log_sigmoid(soft_cap(gates[..., M_HEADS:] + b_f, M_GATE_CAP))
    h = mlstm_chunkwise(q, k, v, log_i, log_f)
    h = h * lax.rsqrt(jnp.mean(jnp.square(h), -1, keepdims=True) + 1e-6)
    h = h.reshape(bsz, seq, M_HEADS * M_DV) * norm_g * o_gate
    return h.astype(x.dtype) @ w_out


def wkv7_scan(r, w, k, v, a, b):
    bsz = r.shape[0]

    def step(state, inp):
        rt, wt, kt, vt, at, bt = inp
        sa = jnp.einsum('bhvk,bhk->bhv', state, at)
        state = state * wt[:, :, None, :] + sa[..., None] * bt[:, :, None, :] + vt[..., None] * kt[:, :, None, :]
        return state, jnp.einsum('bhvk,bhk->bhv', state, rt)

    init = jnp.zeros((bsz, R_HEADS, R_N, R_N), jnp.float32)
    xs = (jnp.swapaxes(r, 0, 1), jnp.swapaxes(w, 0, 1), jnp.swapaxes(k, 0, 1),
          jnp.swapaxes(v, 0, 1), jnp.swapaxes(a, 0, 1), jnp.swapaxes(b, 0, 1))
    _, y = lax.scan(step, init, xs)
    return jnp.swapaxes(y, 0, 1)


def rwkv7_mixer(x, w_in, mu, w0, w2, a0, a2, g2, k_k, k_a, r_k, gn_g, gn_b, w_out):
    bsz, seq, d = x.shape
    proj = x @ w_in
    proj = (proj + mu * (token_shift(proj) - proj)).astype(jnp.float32)
    r = proj[..., :d]
    k = proj[..., d:2 * d]
    v = proj[..., 2 * d:3 * d]
    xw = proj[..., 3 * d:3 * d + R_LW]
    xa = proj[..., 3 * d + R_LW:3 * d + R_LW + R_LA]
    xg = proj[..., 3 * d + R_LW + R_LA:]
    log_w = -jax.nn.softplus(-(w0 + jnp.tanh(xw) @ w2)) - 0.5
    decay = jnp.exp(-jnp.exp(log_w))
    a = jax.nn.sigmoid(a0 + xa @ a2)
    g = jax.nn.sigmoid(xg) @ g2
    kk = (k * k_k).reshape(bsz, seq, R_HEADS, R_N)
    kk = kk / jnp.maximum(jnp.sqrt(jnp.sum(jnp.square(kk), -1, keepdims=True)), 1e-12)
    k = k * (1.0 + (a - 1.0) * k_a)
    r_h = r.reshape(bsz, seq, R_HEADS, R_N)
    k_h = k.reshape(bsz, seq, R_HEADS, R_N)
    v_h = v.reshape(bsz, seq, R_HEADS, R_N)
    a_h = a.reshape(bsz, seq, R_HEADS, R_N)
    w_h = decay.reshape(bsz, seq, R_HEADS, R_N)
    y = wkv7_scan(r_h, w_h, k_h, v_h, -kk, kk * a_h)
    y_mu = jnp.mean(y, -1, keepdims=True)
    y_var = jnp.mean(jnp.square(y - y_mu), -1, keepdims=True)
    y = ((y - y_mu) * lax.rsqrt(y_var + R_GN_EPS)).reshape(bsz, seq, d) * gn_g + gn_b
    bonus = jnp.sum(r_h * k_h * r_k, -1, keepdims=True) * v_h
    y = (y + bonus.reshape(bsz, seq, d)) * g
    return y.astype(x.dtype) @ w_out


def squared_relu_mlp(x, w1, w2):
    return jnp.square(jax.nn.relu(x @ w1)) @ w2


def setup_inputs(seed: int = 0) -> dict:
    key = jax.random.key(seed)
    ks = jax.random.split(key, 28)
    f32 = jnp.float32
    nm, nr, d = N_MLSTM_LAYERS, N_RWKV_LAYERS, D_MODEL

    def normal(k, shape, s):
        return s * jax.random.normal(k, shape, f32)

    return {
        'x': normal(ks[0], (BATCH, SEQ, d), 1.0),
        'mlstm_w_in': normal(ks[1], (nm, d, M_PROJ), d ** -0.5),
        'mlstm_b_i': normal(ks[2], (nm, M_HEADS), 0.1),
        'mlstm_b_f': jnp.linspace(3.0, 6.0, M_HEADS, dtype=f32) + normal(ks[3], (nm, M_HEADS), 0.1),
        'mlstm_conv_w': normal(ks[4], (nm, M_CONV, M_QK), M_CONV ** -0.5),
        'mlstm_conv_b': normal(ks[5], (nm, M_QK), 0.01),
        'mlstm_norm_g': 1.0 + normal(ks[6], (nm, M_HEADS * M_DV), 0.01),
        'mlstm_w_out': normal(ks[7], (nm, M_HEADS * M_DV, d), DN_BETA * (M_HEADS * M_DV) ** -0.5),
        'rwkv_w_in': normal(ks[8], (nr, d, R_PROJ), d ** -0.5),
        'rwkv_mu': jax.random.uniform(ks[9], (nr, R_PROJ), f32),
        'rwkv_w0': jnp.linspace(-6.5, -1.5, d, dtype=f32) + normal(ks[10], (nr, d), 0.1),
        'rwkv_w2': normal(ks[11], (nr, R_LW, d), 0.5 * R_LW ** -0.5),
        'rwkv_a0': normal(ks[12], (nr, d), 0.1),
        'rwkv_a2': normal(ks[13], (nr, R_LA, d), 0.5 * R_LA ** -0.5),
        'rwkv_g2': normal(ks[14], (nr, R_LG, d), R_LG ** -0.5),
        'rwkv_k_k': 0.85 + normal(ks[15], (nr, d), 0.02),
        'rwkv_k_a': 1.0 + normal(ks[16], (nr, d), 0.02),
        'rwkv_r_k': normal(ks[17], (nr, R_HEADS, R_N), 0.1),
        'rwkv_gn_g': 1.0 + normal(ks[18], (nr, d), 0.01),
        'rwkv_gn_b': normal(ks[19], (nr, d), 0.01),
        'rwkv_w_out': normal(ks[20], (nr, d, d), DN_BETA * d ** -0.5),
        'ln_mix_g': 1.0 + normal(ks[21], (DEPTH, d), 0.01),
        'ln_mix_b': normal(ks[22], (DEPTH, d), 0.01),
        'mlp_w1': normal(ks[23], (DEPTH, d, D_FF), d ** -0.5),
        'mlp_w2': normal(ks[24], (DEPTH, D_FF, d), DN_BETA * D_FF ** -0.5),
        'ln_ffn_g': 1.0 + normal(ks[25], (DEPTH, d), 0.01),
        'ln_ffn_b': normal(ks[26], (DEPTH, d), 0.01),
    }


def reference(x, mlstm_w_in, mlstm_b_i, mlstm_b_f, mlstm_conv_w, mlstm_conv_b, mlstm_norm_g, mlstm_w_out,
              rwkv_w_in, rwkv_mu, rwkv_w0, rwkv_w2, rwkv_a0, rwkv_a2, rwkv_g2, rwkv_k_k, rwkv_k_a, rwkv_r_k,
              rwkv_gn_g, rwkv_gn_b, rwkv_w_out, ln_mix_g, ln_mix_b, mlp_w1, mlp_w2, ln_ffn_g, ln_ffn_b):
    for layer in range(DEPTH):
        j = layer // N_MIXERS
        if layer % N_MIXERS == 0:
            mix = mlstm_mixer(x, mlstm_w_in[j], mlstm_b_i[j], mlstm_b_f[j], mlstm_conv_w[j], mlstm_conv_b[j],
                              mlstm_norm_g[j], mlstm_w_out[j])
        else:
            mix = rwkv7_mixer(x, rwkv_w_in[j], rwkv_mu[j], rwkv_w0[j], rwkv_w2[j], rwkv_a0[j], rwkv_a2[j],
                              rwkv_g2[j], rwkv_k_k[j], rwkv_k_a[j], rwkv_r_k[j], rwkv_gn_g[j], rwkv_gn_b[j],
                              rwkv_w_out[j])
        x = layer_norm(DN_ALPHA * x + mix, ln_mix_g[layer], ln_mix_b[layer])
        x = layer_norm(DN_ALPHA * x + squared_relu_mlp(x, mlp_w1[layer], mlp_w2[layer]),
                       ln_ffn_g[layer], ln_ffn_b[layer])
    return x
```

```python
import numpy as np
from contextlib import ExitStack
import concourse.bass as bass
import concourse.mybir as mybir
from concourse.bass_utils import run_bass_kernel_spmd

F32 = mybir.dt.float32
BF16 = mybir.dt.bfloat16
AF = mybir.ActivationFunctionType
ALU = mybir.AluOpType
AX = mybir.AxisListType

ENGS = ['pe', 'act', 'dve', 'pool', 'sp']


class Buf:
    def __init__(self, t, name, excl=False):
        self.t = t
        self.name = name
        self.excl = excl

    def __getitem__(self, idx):
        return self.t[idx]


class KB:
    def __init__(self, nc, es, n_dma_sems=20):
        self.nc = nc
        self.es = es
        self.eng = dict(pe=nc.tensor, act=nc.scalar, dve=nc.vector, pool=nc.gpsimd, sp=nc.sync)
        self.sem = {e: es.enter_context(nc.semaphore("s_" + e)) for e in ENGS}
        self.cnt = {e: 0 for e in ENGS}
        self.dsem = [es.enter_context(nc.semaphore("d%d" % i)) for i in range(n_dma_sems)]
        self.dcnt = [0] * n_dma_sems
        self.dnext = 0
        self.waited = {}
        self.state = {}
        self.nbuf = 0

    def sb(self, shape, dtype, name=None):
        self.nbuf += 1
        name = "%s_s%d" % (name or "b", self.nbuf)
        t = self.es.enter_context(self.nc.sbuf_tensor(name, list(shape), dtype))
        return Buf(t, name)

    def ps(self, shape, dtype=F32, name=None):
        self.nbuf += 1
        name = "%s_p%d" % (name or "p", self.nbuf)
        t = self.es.enter_context(self.nc.psum_tensor(name, list(shape), dtype))
        return Buf(t, name, excl=True)

    def _sem(self, s):
        return self.dsem[s[1]] if isinstance(s, tuple) else self.sem[s]

    def _deps(self, e, reads, writes):
        deps = {}

        def add(d):
            if d is None:
                return
            s, c = d
            if deps.get(s, 0) < c:
                deps[s] = c
        for k in reads:
            st = self.state.get(k)
            if st:
                add(st[0])
                kb = k[0] if isinstance(k, tuple) else k
                if getattr(kb, 'excl', False):
                    for s, c in st[1].items():
                        if s != e:
                            add((s, c))
        for k in writes:
            st = self.state.get(k)
            if st:
                add(st[0])
                for s, c in st[1].items():
                    if s == e:
                        continue
                    add((s, c))
        return deps

    def _wait(self, e, deps):
        for s, c in deps.items():
            if s == e and e in ('pe', 'sp'):
                continue
            if self.waited.get((e, s), 0) >= c:
                continue
            self.eng[e].wait_ge(self._sem(s), c)
            self.waited[(e, s)] = c

    def _record(self, src, c, reads, writes):
        for k in reads:
            st = self.state.setdefault(k, [None, {}])
            st[1][src] = c
        for k in writes:
            self.state[k] = [(src, c), {}]

    def op(self, e, fn, reads=(), writes=()):
        deps = self._deps(e, reads, writes)
        self._wait(e, deps)
        ins = fn(self.eng[e])
        self.cnt[e] += 1
        ins.then_inc(self.sem[e], 1)
        self._record(e, self.cnt[e], reads, writes)
        return ins

    def dma(self, e, out, in_, reads=(), writes=(), **kw):
        i = self.dnext
        self.dnext = (i + 1) % len(self.dsem)
        deps = self._deps(e, reads, writes)
        if self.dcnt[i] > 0:
            deps[('d', i)] = max(deps.get(('d', i), 0), self.dcnt[i])
        self._wait(e, deps)
        ins = self.eng[e].dma_start(out=out, in_=in_, **kw)
        self.dcnt[i] += 16
        ins.then_inc(self.dsem[i], 16)
        self._record(('d', i), self.dcnt[i], reads, writes)
        return ins

    def finish(self):
        for i, c in enumerate(self.dcnt):
            if c > 0:
                self.eng['sp'].wait_ge(self.dsem[i], c)
        for e in ENGS:
            if e != 'sp' and self.cnt[e] > 0:
                self.eng['sp'].wait_ge(self.sem[e], self.cnt[e])

    def mm(self, out, lhsT, rhs, start, stop, reads, writes):
        return self.op('pe', lambda pe: pe.matmul(out, lhsT=lhsT, rhs=rhs, start=start, stop=stop),
                       reads=reads, writes=writes)

    def tr(self, out, in_, ident, reads, writes):
        return self.op('pe', lambda pe: pe.transpose(out, in_, ident), reads=reads, writes=writes)

    def act(self, out, in_, func, reads, writes, scale=1.0, bias=0.0, accum_out=None, e='act'):
        def f(a):
            kw = {}
            if accum_out is not None:
                kw['accum_out'] = accum_out
            return a.activation(out=out, in_=in_, func=func, scale=scale, bias=bias, **kw)
        return self.op(e, f, reads=reads, writes=writes)


def _kb_scope(self):
    self._outer = self.es
    self.es = ExitStack()
    return self.es


def _kb_barrier(self):
    for e in ENGS:
        for s in ENGS:
            if s != e and self.cnt[s] > 0 and self.waited.get((e, s), 0) < self.cnt[s]:
                self.eng[e].wait_ge(self.sem[s], self.cnt[s])
                self.waited[(e, s)] = self.cnt[s]
        for i, cv in enumerate(self.dcnt):
            if cv > 0 and self.waited.get((e, ('d', i)), 0) < cv:
                self.eng[e].wait_ge(self.dsem[i], cv)
                self.waited[(e, ('d', i))] = cv


KB.scope = _kb_scope
KB.barrier = _kb_barrier
import math
import numpy as np

ALPHA = (2.0 * 2) ** 0.25
NT = 4096
NTH = 2048
PAIRS = [[0, 1], [2, 3], [4, 5], [6, 7]]


def consts(k):
    c = {}
    idf = k.sb([128, 128], F32, "idf")
    k.op('pool', lambda g: g.memset(idf[:], 1.0), writes=[idf])
    k.op('pool', lambda g: g.affine_select(out=idf[:], in_=idf[:], pattern=[[-1, 128]], compare_op=ALU.is_equal,
                                           fill=0.0, base=0, channel_multiplier=1), reads=[idf], writes=[idf])
    idb = k.sb([128, 128], BF16, "idb")
    k.op('pool', lambda g: g.tensor_copy(out=idb[:], in_=idf[:]), reads=[idf], writes=[idb])
    c['idf'] = idf
    c['idb'] = idb
    nh = k.sb([128, 64], F32, "nhalf")
    k.op('pool', lambda g: g.memset(nh[:], -0.5), writes=[nh])
    c['nhalf'] = nh
    nh5 = k.sb([128, 512], F32, "nhalf512")
    k.op('pool', lambda g: g.memset(nh5[:], -0.5), writes=[nh5])
    c['nhalf512'] = nh5
    return c


def ln_block(k, c, P, src, dstf, dstb, lnb, gi, bi, eps=1e-5):
    idf = c['idf']
    xn = P['xn']
    for t in range(4):
        pb = P['pbig'][t % 2]
        for dc in range(8):
            k.tr(pb[:, dc * 128:(dc + 1) * 128], src[:, dc, t * 128:(t + 1) * 128], idf[:], reads=[src, idf], writes=[pb])
        st = P['st6']
        mv = P['mv']
        k.op('dve', lambda v: v.bn_stats(out=st[:, 0, :], in_=pb[:, 0:512]), reads=[pb], writes=[st])
        k.op('dve', lambda v: v.bn_stats(out=st[:, 1, :], in_=pb[:, 512:1024]), reads=[pb, st], writes=[st])
        k.op('dve', lambda v: v.bn_aggr(out=mv[:], in_=st[:].rearrange("p a b -> p (a b)")), reads=[st], writes=[mv])
        rs = P['rs']
        k.op('dve', lambda v: v.tensor_scalar_add(out=rs[:, 0:1], in0=mv[:, 1:2], scalar1=eps), reads=[mv], writes=[rs])
        k.op('pool', lambda g: g.tensor_tensor(out=rs[:, 1:2], in0=rs[:, 0:1], in1=P['nhalf'][:, 0:1], op=ALU.pow),
             reads=[rs, P['nhalf']], writes=[rs])
        k.op('dve', lambda v: v.scalar_tensor_tensor(out=rs[:, 2:3], in0=mv[:, 0:1], scalar=-1.0, in1=rs[:, 1:2],
                                                     op0=ALU.mult, op1=ALU.mult), reads=[mv, rs], writes=[rs])
        for hf in range(2):
            k.act(xn[:, t, hf * 512:(hf + 1) * 512], pb[:, hf * 512:(hf + 1) * 512], AF.Identity, reads=[pb, rs],
                  writes=[(xn, t)], scale=rs[:, 1:2], bias=rs[:, 2:3])
    for dc in range(8):
        pm = P['psm'][dc % 2]
        for t in range(4):
            k.tr(pm[:, t * 128:(t + 1) * 128], xn[:, t, dc * 128:(dc + 1) * 128], idf[:], reads=[(xn, t), idf], writes=[pm])
        if dstf is not None:
            k.act(dstf[:, dc, :], pm[:, :], AF.Identity, reads=[pm, lnb], writes=[dstf],
                  scale=lnb[:, gi, dc:dc + 1], bias=lnb[:, bi, dc:dc + 1])
        if dstb is not None:
            k.op('dve', lambda v: v.tensor_copy(out=dstb[:, dc, :], in_=dstf[:, dc, :]), reads=[dstf], writes=[dstb])


def mlp_phase(k, c, P, mixg, res_dram, wo_dram, w1b, w2b, lncols, out_f32_dram, out_bf_dram, th, nblk=4, blk0=0):
    wo = P['wo']
    for kc in range(8):
        k.dma('pool', wo[:, kc, :], wo_dram[:, kc, :], writes=[(wo, kc)], max_dma_last_dim=4096)
    lnb, lbase = lncols
    for blk in range(nblk):
        t0 = (blk0 + blk) * 512
        mixT = P['mixT']
        A = P['A']
        Bf = P['Bf']
        Bb = P['Bb']
        Ab = P['Ab']
        hT = P['hT']
        ga, gb_ = mixg(t0)
        k.dma('sp', P['mixA'][:], ga, reads=['mixg'], writes=[P['mixA']])
        k.dma('sp', P['mixB'][:], gb_, reads=['mixg'], writes=[P['mixB']])
        fl = P['flags']
        k.op('dve', lambda v: v.tensor_scalar_mul(out=mixT[:].rearrange("p c t -> p (c t)"), in0=P['mixA'][:].rearrange("p c t -> p (c t)"),
                                                  scalar1=fl[:, 0:1]), reads=[P['mixA'], fl], writes=[mixT])
        k.op('dve', lambda v: v.scalar_tensor_tensor(out=mixT[:].rearrange("p c t -> p (c t)"), in0=P['mixB'][:].rearrange("p c t -> p (c t)"),
                                                     scalar=fl[:, 1:2], in1=mixT[:].rearrange("p c t -> p (c t)"), op0=ALU.mult, op1=ALU.add),
             reads=[P['mixB'], fl, mixT], writes=[mixT])
        k.dma('sp', A[:], res_dram(t0), reads=['resd'], writes=[A])
        for dc in range(8):
            pm = P['psm'][dc % 2]
            for kc in range(8):
                k.mm(pm[:, :], wo[:, kc, dc * 128:(dc + 1) * 128], mixT[:, kc, :], kc == 0, kc == 7,
                     reads=[(wo, kc), mixT], writes=[pm])
            k.op('dve', lambda v: v.scalar_tensor_tensor(out=A[:, dc, :], in0=A[:, dc, :], scalar=ALPHA, in1=pm[:, :],
                                                         op0=ALU.mult, op1=ALU.add), reads=[A, pm], writes=[A])
        ln_block(k, c, P, A, Bf, Bb, lnb, lbase, lbase + 1)
        for fg in range(8):
            w1g = P['w1g'][fg % 2]
            k.dma('sp', w1g[:], w1b[fg], reads=[('w1b', fg)], writes=[w1g])
            for j in range(4):
                fc = fg * 4 + j
                pm = P['psm'][2 + fc % 2]
                for kc in range(8):
                    k.mm(pm[:, :], w1g[:, kc, j * 128:(j + 1) * 128], Bb[:, kc, :], kc == 0, kc == 7,
                         reads=[w1g, Bb], writes=[pm])
                r = P['rt'][fc % 2]
                k.act(r[:], pm[:, :], AF.Relu, reads=[pm], writes=[r])
                k.op('pool', lambda g: g.tensor_tensor(out=hT[:, fc, :], in0=r[:], in1=r[:], op=ALU.mult),
                     reads=[r], writes=[(hT, fc)])
        for dc in range(8):
            w2c = P['w2c'][dc % 2]
            k.dma('sp', w2c[:], w2b[dc], reads=[('w2b', dc)], writes=[w2c])
            pm = P['psm'][dc % 2]
            for fc in range(32):
                k.mm(pm[:, :], w2c[:, fc, :], hT[:, fc, :], fc == 0, fc == 31, reads=[w2c, (hT, fc)], writes=[pm])
            k.op('dve', lambda v: v.scalar_tensor_tensor(out=Bf[:, dc, :], in0=Bf[:, dc, :], scalar=ALPHA, in1=pm[:, :],
                                                         op0=ALU.mult, op1=ALU.add), reads=[Bf, pm], writes=[Bf])
        ln_block(k, c, P, Bf, A, Ab if out_bf_dram is not None else None, lnb, lbase + 2, lbase + 3)
        k.dma('sp', out_f32_dram(t0), A[:], reads=[A], writes=['outf'])
        if out_bf_dram is not None:
            for j, ap_ in enumerate(out_bf_dram(t0)):
                k.dma('sp', ap_, Ab[:, j * 4:(j + 1) * 4, :], reads=[Ab], writes=['outb'])


def alloc_mlp(k, c):
    P = {}
    P['wo'] = k.sb([128, 8, 1024], BF16, "wo")
    P['mixT'] = k.sb([128, 8, 512], BF16, "mixT")
    P['mixA'] = k.sb([128, 8, 512], BF16, "mixA")
    P['mixB'] = k.sb([128, 8, 512], BF16, "mixB")
    P['flags'] = k.sb([128, 2], F32, "flags")
    P['A'] = k.sb([128, 8, 512], F32, "A")
    P['Bf'] = k.sb([128, 8, 512], F32, "Bf")
    P['Bb'] = k.sb([128, 8, 512], BF16, "Bb")
    P['Ab'] = k.sb([128, 8, 512], BF16, "Ab")
    P['hT'] = k.sb([128, 32, 512], BF16, "hT")
    P['xn'] = k.sb([128, 4, 1024], F32, "xn")
    P['w1g'] = [k.sb([128, 8, 512], BF16, "w1g%d" % i) for i in range(2)]
    P['w2c'] = [k.sb([128, 32, 128], BF16, "w2c%d" % i) for i in range(2)]
    P['rt'] = [k.sb([128, 512], F32, "rt%d" % i) for i in range(2)]
    P['st6'] = k.sb([128, 2, 6], F32, "st6")
    P['mv'] = k.sb([128, 2], F32, "mv")
    P['rs'] = k.sb([128, 4], F32, "rs")
    P['nhalf'] = c['nhalf']
    return P


LN_C1 = math.log(0.25 / math.sqrt(128.0))
LN_C2 = math.log(0.5 / math.sqrt(128.0))
LN_HALF = math.log(0.5)


def alloc_mlstm(k):
    M = {}
    M["xT"] = k.sb([128, 8, NT], BF16, "xT_sb")
    M['win'] = k.sb([128, 8, 1544], BF16, "win")
    M['gb'] = k.sb([128, 4], F32, "gb")
    M['cw'] = k.sb([128, 4, 4], F32, "cw")
    M['cb'] = k.sb([128, 4], F32, "cb")
    M['ng'] = k.sb([128, 512], F32, "ng")
    M['triu'] = k.sb([128, 128], F32, "triu")
    M['ones'] = k.sb([128, 128], F32, "onesf")
    M['maskneg'] = k.sb([128, 128], F32, "maskneg")
    M['G'] = k.sb([128, 32, 4], F32, "G")
    M['th'] = k.sb([128, 32, 4], F32, "thg")
    M['lf'] = k.sb([128, 32, 2], F32, "lf")
    M['bias_tab'] = k.sb([128, 32, 2], F32, "bias_tab")
    M['wk_tab'] = k.sb([128, 32, 2], F32, "wk_tab")
    M['expA_tab'] = k.sb([128, 32, 2], F32, "expA_tab")
    M['tmp64'] = k.sb([128, 32, 2], F32, "tmp64")
    M['pre'] = [k.sb([128, 515], F32, "pre%d" % i) for i in range(4)]
    M['acc'] = k.sb([128, 512], F32, "acc")
    M['tht'] = k.sb([128, 512], F32, "tht")
    M['qk'] = [k.sb([128, 512], BF16, "qk%d" % i) for i in range(4)]
    M['V1'] = k.sb([128, 4, 2, 257], BF16, "V1")
    M['og'] = k.sb([128, 4, 512], F32, "og")
    M['tmpA'] = k.sb([128, 128], F32, "tmpA")
    M['DT'] = k.sb([128, 128], F32, "DT")
    M['Gbc'] = k.sb([128, 128], F32, "Gbc")
    M['ST'] = k.sb([128, 128], BF16, "ST")
    M['Qg'] = k.sb([128, 128], BF16, "Qg")
    M['Vw'] = k.sb([128, 257], BF16, "Vw")
    M['ktok'] = k.sb([128, 128], BF16, "ktok")
    M['Cf'] = [k.sb([128, 257], F32, "Cf%d" % i) for i in range(2)]
    M['Cb'] = [k.sb([128, 257], BF16, "Cb%d" % i) for i in range(2)]
    M['num'] = k.sb([128, 8, 256], F32, "num")
    M['den'] = k.sb([128, 8], F32, "den")
    M['ss'] = k.sb([128, 8], F32, "ss")
    M['sq'] = k.sb([128, 256], F32, "sqjunk")
    M['stt'] = k.sb([128, 4, 8], F32, "stt")
    M['hg'] = k.sb([128, 4, 512], BF16, "hg")
    M['hgT'] = k.sb([128, 4, 512], BF16, "hgT")
    return M


def mlstm_phase(k, c, M, PS, xT_d, win_d, gb_d, cw_d, cb_d, ng_d, mix_out, nblk=8):
    idb = c['idb']
    xT = M['xT']; win = M['win']
    for kc in range(8):
        for hf in range(2):
            k.dma('pool', xT[:, kc, hf * 2048:(hf + 1) * 2048], xT_d[kc * 128:(kc + 1) * 128, hf * 2048:(hf + 1) * 2048],
                  writes=[(xT, kc, hf)], max_dma_last_dim=4096)
        k.dma('pool', win[:, kc, :], win_d[:, kc, :], writes=[(win, kc)], max_dma_last_dim=4096)
    for name, d in (('gb', gb_d), ('cw', cw_d), ('cb', cb_d), ('ng', ng_d)):
        k.dma('sp', M[name][:], d, writes=[M[name]])
    triu = M['triu']; ones = M['ones']; mneg = M['maskneg']
    k.op('pool', lambda g: g.memset(ones[:], 1.0), writes=[ones])
    k.op('pool', lambda g: g.memset(triu[:], 1.0), writes=[triu])
    k.op('pool', lambda g: g.affine_select(out=triu[:], in_=triu[:], pattern=[[1, 128]], compare_op=ALU.is_ge, fill=0.0,
                                           base=0, channel_multiplier=-1), reads=[triu], writes=[triu])
    k.op('pool', lambda g: g.memset(mneg[:], 0.0), writes=[mneg])
    k.op('pool', lambda g: g.affine_select(out=mneg[:], in_=mneg[:], pattern=[[1, 128]], compare_op=ALU.is_ge, fill=-30000.0,
                                           base=0, channel_multiplier=-1), reads=[mneg], writes=[mneg])
    V1 = M['V1']
    k.op('pool', lambda g: g.memset(V1[:], 1.0), writes=[(V1, t) for t in range(4)])
    for h in range(2):
        k.op('pool', lambda g: g.memset(M['Cf'][h][:], 0.0), writes=[M['Cf'][h]])
        k.op('pool', lambda g: g.memset(M['Cb'][h][:], 0.0), writes=[M['Cb'][h]])
    for i in range(4):
        k.op('pool', lambda g: g.memset(M['pre'][i][:, 0:3], 0.0), writes=[M['pre'][i]])
    xrd = [(xT, kc, hf) for kc in range(8) for hf in range(2)]
    wrd = [(win, kc) for kc in range(8)]

    ntile = nblk * 4
    pg = PS['psm'][0]
    for t in range(ntile):
        for kc in range(8):
            k.mm(pg[:, t * 4:(t + 1) * 4], xT[:, kc, t * 128:(t + 1) * 128], win[:, kc, 1536:1540], kc == 0, kc == 7,
                 reads=[(xT, kc, t // 16), (win, kc)], writes=[pg])
    G = M['G']; th = M['th']; lf = M['lf']
    nt_ = ntile
    k.op('dve', lambda v: v.tensor_tensor(out=G[:, 0:nt_, :], in0=pg[:, 0:nt_ * 4].rearrange("p (t g) -> p t g", g=4),
                                          in1=M['gb'][:, :].unsqueeze(1).to_broadcast([128, nt_, 4]), op=ALU.add),
         reads=[pg, M['gb']], writes=[G])
    k.act(th[:, 0:nt_, :], G[:, 0:nt_, :], AF.Tanh, reads=[G], writes=[th], scale=1.0 / 15.0)
    k.act(G[:, 0:nt_, 2:4], th[:, 0:nt_, 2:4], AF.Exp, reads=[th], writes=[G], scale=-15.0)
    k.act(G[:, 0:nt_, 2:4], G[:, 0:nt_, 2:4], AF.Ln, reads=[G], writes=[G], bias=1.0)
    k.op('dve', lambda v: v.tensor_scalar_mul(out=lf[:, 0:nt_, :], in0=G[:, 0:nt_, 2:4], scalar1=-1.0), reads=[G], writes=[lf])
    pa = PS['psm'][1]; pt = PS['psm'][2]
    lf2 = lf[:, 0:nt_, :].rearrange("p t h -> p (t h)")
    k.mm(pa[:, 0:nt_ * 2], triu[:], lf2, True, True, reads=[triu, lf], writes=[pa])
    k.mm(pt[:, 0:nt_ * 2], ones[:], lf2, True, True, reads=[ones, lf], writes=[pt])
    bt = M['bias_tab']; wk = M['wk_tab']; ea = M['expA_tab']; tmp = M['tmp64']
    bt2 = bt[:, 0:nt_, :].rearrange("p t h -> p (t h)")
    k.op('dve', lambda v: v.scalar_tensor_tensor(out=bt[:, 0:nt_, :], in0=th[:, 0:nt_, 0:2], scalar=15.0,
                                                 in1=pa[:, 0:nt_ * 2].rearrange("p (t h) -> p t h", h=2),
                                                 op0=ALU.mult, op1=ALU.subtract), reads=[th, pa], writes=[bt])
    k.op('dve', lambda v: v.tensor_tensor(out=tmp[:, 0:nt_, :].rearrange("p t h -> p (t h)"), in0=bt2, in1=pt[:, 0:nt_ * 2], op=ALU.add),
         reads=[bt, pt], writes=[tmp])
    k.act(wk[:, 0:nt_, :], tmp[:, 0:nt_, :], AF.Exp, reads=[tmp], writes=[wk], bias=LN_HALF)
    k.act(ea[:, 0:nt_, :].rearrange("p t h -> p (t h)"), pt[:, 0:nt_ * 2], AF.Exp, reads=[pt], writes=[ea])
    k.op('dve', lambda v: v.tensor_scalar_add(out=bt2, in0=bt2, scalar1=LN_C1), reads=[bt], writes=[bt])

    for blk in range(nblk):
        t0 = blk * 512
        hf = blk // 4
        for oc in range(4):
            pm = PS['psm'][oc % 2]
            for kc in range(8):
                k.mm(pm[:, :], win[:, kc, oc * 128:(oc + 1) * 128], xT[:, kc, t0:t0 + 512], kc == 0, kc == 7,
                     reads=[(xT, kc, hf), (win, kc)], writes=[pm])
            pre = M['pre'][oc]; acc = M['acc']; tht = M['tht']; cw = M['cw']; cb = M['cb']
            k.act(pre[:, 3:515], pm[:, :], AF.Copy, reads=[pm], writes=[pre])
            k.op('dve', lambda v: v.tensor_scalar(out=acc[:], in0=pre[:, 0:512], scalar1=cw[:, oc, 0:1], scalar2=cb[:, oc:oc + 1],
                                                  op0=ALU.mult, op1=ALU.add), reads=[pre, cw, cb], writes=[acc])
            for j in range(1, 4):
                k.op('dve', lambda v: v.scalar_tensor_tensor(out=acc[:], in0=pre[:, j:j + 512], scalar=cw[:, oc, j:j + 1], in1=acc[:],
                                                             op0=ALU.mult, op1=ALU.add), reads=[pre, cw, acc], writes=[acc])
            k.act(pre[:, 0:3], pre[:, 512:515], AF.Copy, reads=[pre], writes=[pre])
            k.act(tht[:], acc[:], AF.Tanh, reads=[acc], writes=[tht], scale=0.5)
            k.op('dve', lambda v: v.scalar_tensor_tensor(out=M['qk'][oc][:], in0=tht[:], scalar=1.0, in1=acc[:],
                                                         op0=ALU.add, op1=ALU.mult), reads=[tht, acc], writes=[M['qk'][oc]])
        og = M['og']
        for t in range(4):
            tk = t0 + t * 128
            pv = PS['psm'][2]; po = PS['psm'][3]
            for kc in range(8):
                k.mm(pv[:, :], xT[:, kc, tk:tk + 128], win[:, kc, 512:1024], kc == 0, kc == 7,
                     reads=[(xT, kc, hf), (win, kc)], writes=[pv])
            k.act(V1[:, t, :, 0:256], pv[:, :].rearrange("p (h e) -> p h e", h=2), AF.Copy, reads=[pv], writes=[(V1, t)])
            for kc in range(8):
                k.mm(po[:, :], xT[:, kc, tk:tk + 128], win[:, kc, 1024:1536], kc == 0, kc == 7,
                     reads=[(xT, kc, hf), (win, kc)], writes=[po])
            tht = M['tht']
            k.act(tht[:], po[:, :], AF.Tanh, reads=[po], writes=[tht], scale=0.5)
            k.op('dve', lambda v: v.scalar_tensor_tensor(out=og[:, t, :], in0=tht[:], scalar=1.0, in1=M['ng'][:],
                                                         op0=ALU.add, op1=ALU.mult), reads=[tht, M['ng']], writes=[(og, t)])
        for t in range(4):
            cg = blk * 4 + t
            for h in range(2):
                slot = t * 2 + h
                qT = M['qk'][h]; kT = M['qk'][2 + h]
                pkq = PS['psm'][0]; pab = PS['psm'][1]; pnd = PS['psm'][2]; pst = PS['psm'][3]; pkt = PS['pbig'][0]
                k.mm(pkq[:, 0:128], kT[:, t * 128:(t + 1) * 128], qT[:, t * 128:(t + 1) * 128], True, True,
                     reads=[kT, qT], writes=[pkq])
                k.mm(pab[:, 0:128], lf[:, cg, h:h + 1].to_broadcast([128, 128]), triu[:], True, True,
                     reads=[lf, triu], writes=[pab])
                k.mm(pkt[:, 0:128], kT[:, t * 128:(t + 1) * 128], idb[:], True, True, reads=[kT, idb], writes=[pkt])
                tmpA = M['tmpA']; DT = M['DT']; Gbc = M['Gbc']; ST = M['ST']; Qg = M['Qg']; Vw = M['Vw']; ktok = M['ktok']
                k.op('dve', lambda v: v.tensor_tensor(out=tmpA[:], in0=pab[:, 0:128], in1=mneg[:], op=ALU.add),
                     reads=[pab, mneg], writes=[tmpA])
                k.act(Gbc[:], pab[:, 0:128], AF.Exp, reads=[pab], writes=[Gbc], bias=LN_C2)
                k.act(DT[:], tmpA[:], AF.Exp, reads=[tmpA, bt], writes=[DT], bias=bt[:, cg, h:h + 1])
                k.op('dve', lambda v: v.tensor_tensor(out=ST[:], in0=pkq[:, 0:128], in1=DT[:], op=ALU.mult),
                     reads=[pkq, DT], writes=[ST])
                k.op('dve', lambda v: v.tensor_tensor(out=Qg[:], in0=qT[:, t * 128:(t + 1) * 128], in1=Gbc[:], op=ALU.mult),
                     reads=[qT, Gbc], writes=[Qg])
                k.act(Vw[:], V1[:, t, h, :], AF.Identity, reads=[(V1, t), wk], writes=[Vw], scale=wk[:, cg, h:h + 1])
                k.act(ktok[:], pkt[:, 0:128], AF.Copy, reads=[pkt], writes=[ktok])
                Cf = M['Cf'][h]; Cb = M['Cb'][h]
                k.mm(pnd[:, 0:257], ST[:], V1[:, t, h, :], True, False, reads=[ST, (V1, t)], writes=[pnd])
                k.mm(pnd[:, 0:257], Qg[:], Cb[:], False, True, reads=[Qg, Cb], writes=[pnd])
                k.mm(pst[:, 0:257], ktok[:], Vw[:], True, True, reads=[ktok, Vw], writes=[pst])
                k.op('dve', lambda v: v.scalar_tensor_tensor(out=Cf[:], in0=Cf[:], scalar=ea[:, cg, h:h + 1], in1=pst[:, 0:257],
                                                             op0=ALU.mult, op1=ALU.add), reads=[Cf, ea, pst], writes=[Cf])
                k.act(Cb[:], Cf[:], AF.Copy, reads=[Cf], writes=[Cb])
                num = M['num']; den = M['den']; ss = M['ss']
                k.act(M['sq'][:], pnd[:, 0:256], AF.Square, reads=[pnd], writes=[M['sq'], (ss, slot)], accum_out=ss[:, slot:slot + 1])
                k.act(num[:, slot, :], pnd[:, 0:256], AF.Copy, reads=[pnd], writes=[(num, slot)])
                k.act(den[:, slot:slot + 1], pnd[:, 256:257], AF.Copy, reads=[pnd], writes=[(den, slot)])
        stt = M['stt']; den = M['den']; ss = M['ss']; num = M['num']; hg = M['hg']
        rdl = [(den, s) for s in range(8)] + [(ss, s) for s in range(8)]
        k.op('dve', lambda v: v.scalar_tensor_tensor(out=stt[:, 0, :], in0=den[:], scalar=-1.0, in1=den[:], op0=ALU.mult, op1=ALU.max),
             reads=rdl, writes=[stt])
        k.op('dve', lambda v: v.tensor_scalar_max(out=stt[:, 0, :], in0=stt[:, 0, :], scalar1=1.0), reads=[stt], writes=[stt])
        k.op('dve', lambda v: v.reciprocal(out=stt[:, 1, :], in_=stt[:, 0, :]), reads=[stt], writes=[stt])
        k.op('dve', lambda v: v.tensor_tensor(out=stt[:, 2, :], in0=stt[:, 1, :], in1=stt[:, 1, :], op=ALU.mult), reads=[stt], writes=[stt])
        k.op('dve', lambda v: v.scalar_tensor_tensor(out=stt[:, 2, :], in0=ss[:], scalar=1.0 / 256.0, in1=stt[:, 2, :],
                                                     op0=ALU.mult, op1=ALU.mult), reads=rdl + [stt], writes=[stt])
        k.op('dve', lambda v: v.tensor_scalar_add(out=stt[:, 2, :], in0=stt[:, 2, :], scalar1=1e-6), reads=[stt], writes=[stt])
        k.op('pool', lambda g: g.tensor_tensor(out=stt[:, 3, :], in0=stt[:, 2, :], in1=c['nhalf'][:, 0:8], op=ALU.pow),
             reads=[stt, c['nhalf']], writes=[stt])
        k.op('dve', lambda v: v.scalar_tensor_tensor(out=stt[:, 3, :], in0=stt[:, 3, :], scalar=0.5, in1=stt[:, 1, :],
                                                     op0=ALU.mult, op1=ALU.mult), reads=[stt], writes=[stt])
        for t in range(4):
            for h in range(2):
                slot = t * 2 + h
                k.op('dve', lambda v: v.scalar_tensor_tensor(out=hg[:, t, h * 256:(h + 1) * 256], in0=num[:, slot, :],
                                                             scalar=stt[:, 3, slot:slot + 1], in1=og[:, t, h * 256:(h + 1) * 256],
                                                             op0=ALU.mult, op1=ALU.mult),
                     reads=[(num, slot), stt, (og, t)], writes=[(hg, t)])
        hgT = M['hgT']
        for cc in range(4):
            pm = PS['psm'][cc % 2]
            for t in range(4):
                k.mm(pm[:, t * 128:(t + 1) * 128], hg[:, t, cc * 128:(cc + 1) * 128], idb[:], True, True,
                     reads=[(hg, t), idb], writes=[pm])
            k.act(hgT[:, cc, :], pm[:, :], AF.Copy, reads=[pm], writes=[hgT])
        k.dma('sp', mix_out(blk), hgT[:], reads=[hgT], writes=['mixo'])


C_LW = -0.5 * math.exp(-0.5)


def alloc_rwkv(k):
    R = {}
    xb0 = k.sb([128, 8, 512], BF16, "r_xb0")
    R['xb'] = [xb0, xb0]
    R['win'] = k.sb([128, 8, 1792], BF16, "r_win")
    R['mu'] = k.sb([128, 14], F32, "r_mu")
    R['chp'] = k.sb([128, 5, 4], F32, "r_chp")
    R['chq'] = k.sb([128, 5, 4], F32, "r_chq")
    R['w2a2'] = k.sb([128, 512], BF16, "r_w2a2")
    R['g2'] = k.sb([128, 512], BF16, "r_g2")
    R['gn'] = k.sb([128, 2, 4, 64], F32, "r_gn")
    R['plast'] = k.sb([128, 14], F32, "r_plast")
    R['raw'] = [k.sb([128, 513], F32, "r_raw%d" % i) for i in range(2)]
    R['tmp'] = [k.sb([128, 512], F32, "r_tmp%d" % i) for i in range(2)]
    for n in ('Pr', 'Pk', 'Pv', 'P12', 'P13', 'lw', 'lc', 'E1', 'E2', 'E3', 'ta', 'rn', 'kkh', 'km'):
        R[n] = k.sb([128, 512], F32, "r_" + n)
    R['bs'] = R['rn']
    R['u'] = R['lw']
    R['wab'] = k.sb([128, 512], BF16, "r_wab")
    R['sgb'] = k.sb([128, 512], BF16, "r_sgb")
    R['sqb'] = k.sb([128, 512], BF16, "r_sqb")
    for n in ('Af', 'Bf', 'Kf', 'Rf', 'rkm', 'Vb'):
        R[n] = [k.sb([128, 512], BF16, "r_%s%d" % (n, i)) for i in range(4)]
    R['cL'] = k.sb([128, 4, 8], F32, "r_cL")
    R['tokb'] = k.sb([128, 4, 8, 4, 64], BF16, "r_tokb")
    R['vtok'] = k.sb([128, 4, 8, 64], F32, "r_vtok")
    R['amU'] = k.sb([128, 4, 8, 4, 64], BF16, "r_amU")
    R['NM'] = [k.sb([128, 8, 2, 64], BF16, "r_NM%d" % i) for i in range(2)]
    R['Xb'] = [k.sb([128, 8, 128], BF16, "r_Xb%d" % i) for i in range(2)]
    R['TAT'] = k.sb([128, 4, 8, 64], BF16, "r_TAT")
    R['TAV'] = k.sb([128, 4, 8, 64], F32, "r_TAV")
    R['Ysb'] = k.sb([128, 4, 8, 64], F32, "r_Ysb")
    R['S'] = k.sb([128, 4, 64], F32, "r_S")
    R['St'] = k.sb([128, 4, 64], F32, "r_St")
    R['Sb'] = k.sb([128, 4, 64], BF16, "r_Sb")
    R['Ub'] = k.sb([128, 4, 64], BF16, "r_Ub")
    R['maskU'] = k.sb([128, 2, 4, 64], F32, "r_maskU")
    R['maskL'] = k.sb([128, 8, 64], F32, "r_maskL")
    R['rmask'] = k.sb([128, 512], F32, "r_rmask")
    R['bones'] = k.sb([128, 128], BF16, "r_bones")
    R['zb'] = k.sb([128, 512], BF16, "r_zb")
    R['onesb'] = k.sb([128, 1], BF16, "r_onesb")
    R['rk'] = k.sb([128, 4, 8], F32, "r_rk")
    R['gst'] = k.sb([128, 6, 32], F32, "r_gst")
    R['Yt'] = k.sb([128, 4, 8, 64], F32, "r_Yt")
    R['ygb'] = k.sb([128, 4, 8, 64], BF16, "r_ygb")
    R['ygT'] = k.sb([128, 4, 512], BF16, "r_ygT")
    return R


def rwkv_phase(k, c, R, PS, x_src, win_d, mu_d, chp_d, w2a2_d, g2_d, gn_d, mix_out, nblk=8):
    idb = c['idb']
    win = R['win']
    for kc in range(8):
        k.dma('pool', win[:, kc, :], win_d[:, kc, :], writes=[(win, kc)], max_dma_last_dim=4096)
    k.dma('pool', R['w2a2'][:], w2a2_d, writes=[R['w2a2']], max_dma_last_dim=2048)
    k.dma('pool', R['g2'][:], g2_d, writes=[R['g2']], max_dma_last_dim=2048)
    k.dma('sp', R['mu'][:], mu_d, writes=[R['mu']])
    k.dma('sp', R['chp'][:], chp_d, writes=[R['chp']])
    k.dma('sp', R['gn'][:], gn_d, writes=[R['gn']])
    chp = R['chp']; chq = R['chq']
    k.op('dve', lambda v: v.tensor_scalar_mul(out=chq[:, 0:4, :], in0=chp[:, 0:4, :], scalar1=0.5), reads=[chp], writes=[chq])
    k.op('dve', lambda v: v.tensor_scalar(out=chq[:, 4, :], in0=chp[:, 3, :], scalar1=-0.5, scalar2=1.0, op0=ALU.mult, op1=ALU.add),
         reads=[chp, chq], writes=[chq])
    mU = R['maskU']; mL = R['maskL']; rm = R['rmask']; bo = R['bones']
    k.op('pool', lambda g: g.memset(mU[:], 1.0), writes=[mU])
    for e in range(2):
        sl = slice(e * 64, (e + 1) * 64)
        k.op('pool', lambda g: g.affine_select(out=mU[sl, :, 0:2, :], in_=mU[sl, :, 0:2, :], pattern=[[0, 2], [0, 2], [1, 64]],
                                               compare_op=ALU.is_gt, fill=0.0, base=0, channel_multiplier=-1), reads=[mU], writes=[mU])
        k.op('pool', lambda g: g.affine_select(out=mU[sl, :, 2:4, :], in_=mU[sl, :, 2:4, :], pattern=[[0, 2], [0, 2], [1, 64]],
                                               compare_op=ALU.is_ge, fill=0.0, base=0, channel_multiplier=-1), reads=[mU], writes=[mU])
    k.op('pool', lambda g: g.memset(mL[:], 1.0), writes=[mL])
    for e in range(2):
        sl = slice(e * 64, (e + 1) * 64)
        k.op('pool', lambda g: g.affine_select(out=mL[sl, :, :], in_=mL[sl, :, :], pattern=[[0, 8], [-1, 64]],
                                               compare_op=ALU.is_gt, fill=0.0, base=0, channel_multiplier=1), reads=[mL], writes=[mL])
    k.op('pool', lambda g: g.memset(rm[:], 1.0), writes=[rm])
    k.op('pool', lambda g: g.memset(rm[:].rearrange("p (c l) -> p c l", l=64)[:, :, 0:1], 0.0), reads=[rm], writes=[rm])
    k.op('pool', lambda g: g.memset(bo[:], 0.0), writes=[bo])
    for e in range(2):
        k.op('pool', lambda g: g.memset(bo[e * 64:(e + 1) * 64, e * 64:(e + 1) * 64], 1.0), reads=[bo], writes=[bo])
    k.op('pool', lambda g: g.memset(R['zb'][:], 0.0), writes=[R['zb']])
    k.op('pool', lambda g: g.memset(R['onesb'][:], 1.0), writes=[R['onesb']])
    k.op('pool', lambda g: g.memset(R['plast'][:], 0.0), writes=[R['plast']])
    k.op('pool', lambda g: g.memset(R['S'][:], 0.0), writes=[R['S']])
    k.op('pool', lambda g: g.memset(R['Sb'][:], 0.0), writes=[R['Sb']])
    wrd = [(win, kc) for kc in range(8)]
    psm = PS['psm']; pbig = PS['pbig']
    mu = R['mu']; plast = R['plast']

    def project(oc, xb, dst, nproj):
        pm = psm[nproj % 2]
        for kc in range(8):
            k.mm(pm[:, :], win[:, kc, oc * 128:(oc + 1) * 128], xb[:, kc, :], kc == 0, kc == 7, reads=[(win, kc), xb], writes=[pm])
        raw = R['raw'][nproj % 2]; tmp = R['tmp'][nproj % 2]
        k.act(raw[:, 1:513], pm[:, :], AF.Copy, reads=[pm], writes=[raw])
        k.act(raw[:, 0:1], plast[:, oc:oc + 1], AF.Copy, reads=[plast, raw], writes=[raw])
        k.act(plast[:, oc:oc + 1], raw[:, 512:513], AF.Copy, reads=[raw], writes=[plast])
        k.op('dve', lambda v: v.tensor_tensor(out=tmp[:], in0=raw[:, 0:512], in1=pm[:, :], op=ALU.subtract), reads=[raw, pm], writes=[tmp])
        k.op('dve', lambda v: v.scalar_tensor_tensor(out=dst[:], in0=tmp[:], scalar=mu[:, oc:oc + 1], in1=pm[:, :],
                                                     op0=ALU.mult, op1=ALU.add), reads=[tmp, mu, pm], writes=[dst])

    for blk in range(nblk):
        xb = R['xb'][blk % 2]
        for j, ap_ in enumerate(x_src(blk)):
            k.dma('sp', xb[:, j * 4:(j + 1) * 4, :], ap_, reads=['xsrc'], writes=[xb])
        npj = 0
        project(12, xb, R['P12'], npj); npj += 1
        project(13, xb, R['P13'], npj); npj += 1
        wab = R['wab']; sgb = R['sgb']; P12 = R['P12']; P13 = R['P13']
        k.act(wab[0:64, :], P12[0:64, :], AF.Tanh, reads=[P12], writes=[wab])
        k.act(wab[64:128, :], P12[64:128, :], AF.Copy, reads=[P12, wab], writes=[wab])
        k.act(R['u'][:], P13[:], AF.Tanh, reads=[P13], writes=[R['u']], scale=0.5)
        k.op('dve', lambda v: v.tensor_scalar(out=sgb[:], in0=R['u'][:], scalar1=1.0, scalar2=0.5, op0=ALU.add, op1=ALU.mult),
             reads=[R['u']], writes=[sgb])
        for hp in range(4):
            Pr, Pk, Pv = R['Pr'], R['Pk'], R['Pv']
            project(hp, xb, Pr, npj); npj += 1
            project(4 + hp, xb, Pk, npj); npj += 1
            project(8 + hp, xb, Pv, npj); npj += 1
            w2a2 = R['w2a2']
            pzw = psm[2]; pza = psm[3]
            k.mm(pzw[:, :], w2a2[0:64, hp * 128:(hp + 1) * 128], wab[0:64, :], True, True, reads=[w2a2, wab], writes=[pzw])
            k.mm(pza[:, :], w2a2[64:128, hp * 128:(hp + 1) * 128], wab[64:128, :], True, True, reads=[w2a2, wab], writes=[pza])
            lw, lc, E1, E2, E3, ta, rn, kkh, bs, km, u = (R[n] for n in ('lw', 'lc', 'E1', 'E2', 'E3', 'ta', 'rn', 'kkh', 'bs', 'km', 'u'))
            k.act(lw[:], pzw[:, :], AF.Tanh, reads=[pzw, chq], writes=[lw], scale=0.5, bias=chq[:, 0, hp:hp + 1])
            k.act(ta[:], pza[:, :], AF.Tanh, reads=[pza, chq], writes=[ta], scale=0.5, bias=chq[:, 1, hp:hp + 1])
            k.op('dve', lambda v: v.tensor_scalar(out=lw[:], in0=lw[:], scalar1=1.0, scalar2=C_LW, op0=ALU.add, op1=ALU.mult),
                 reads=[lw], writes=[lw])
            k.op('dve', lambda v: v.tensor_tensor_scan(out=lc[:], data0=rm[:], data1=lw[:], initial=0.0, op0=ALU.mult, op1=ALU.add),
                 reads=[rm, lw], writes=[lc])
            k.act(E1[:], lc[:], AF.Exp, reads=[lc], writes=[E1])
            k.act(E2[:], lc[:], AF.Exp, reads=[lc], writes=[E2], scale=-1.0)
            E1v = E1[:].rearrange("p (c l) -> p c l", l=64); E3v = E3[:].rearrange("p (c l) -> p c l", l=64)
            k.op('pool', lambda g: g.tensor_copy(out=E3v[:, :, 1:64], in_=E1v[:, :, 0:63]), reads=[E1], writes=[E3])
            k.op('pool', lambda g: g.memset(E3v[:, :, 0:1], 1.0), reads=[E3], writes=[E3])
            k.act(R['cL'][:, hp, :], E1v[:, :, 63], AF.Copy, reads=[E1], writes=[(R['cL'], hp)])
            sqb = R['sqb']
            k.act(sqb[:], Pk[:], AF.Square, reads=[Pk, chp], writes=[sqb], scale=chp[:, 2, hp:hp + 1])
            pss = psm[2]
            k.mm(pss[:, :], bo[:], sqb[:], True, True, reads=[bo, sqb], writes=[pss])
            k.op('dve', lambda v: v.tensor_scalar_max(out=rn[:], in0=pss[:, :], scalar1=1e-24), reads=[pss], writes=[rn])
            k.op('pool', lambda g: g.tensor_tensor(out=rn[:], in0=rn[:], in1=c['nhalf512'][:], op=ALU.pow), reads=[rn, c['nhalf512']], writes=[rn])
            k.op('dve', lambda v: v.scalar_tensor_tensor(out=kkh[:], in0=Pk[:], scalar=chq[:, 2, hp:hp + 1], in1=rn[:],
                                                         op0=ALU.mult, op1=ALU.mult), reads=[Pk, chq, rn], writes=[kkh])
            Af, Bf, Kf, Rf, rkm, Vb = (R[n][hp] for n in ('Af', 'Bf', 'Kf', 'Rf', 'rkm', 'Vb'))
            k.op('dve', lambda v: v.scalar_tensor_tensor(out=Af[:], in0=kkh[:], scalar=-2.0, in1=E3[:], op0=ALU.mult, op1=ALU.mult),
                 reads=[kkh, E3], writes=[Af])
            k.op('dve', lambda v: v.scalar_tensor_tensor(out=bs[:], in0=ta[:], scalar=1.0, in1=kkh[:], op0=ALU.add, op1=ALU.mult),
                 reads=[ta, kkh], writes=[bs])
            k.op('pool', lambda g: g.tensor_tensor(out=Bf[:], in0=bs[:], in1=E2[:], op=ALU.mult), reads=[bs, E2], writes=[Bf])
            k.act(u[:], ta[:], AF.Identity, reads=[ta, chq], writes=[u], scale=chq[:, 3, hp:hp + 1], bias=chq[:, 4, hp:hp + 1])
            k.op('dve', lambda v: v.tensor_tensor(out=km[:], in0=u[:], in1=Pk[:], op=ALU.mult), reads=[u, Pk], writes=[km])
            k.op('pool', lambda g: g.tensor_tensor(out=Kf[:], in0=km[:], in1=E2[:], op=ALU.mult), reads=[km, E2], writes=[Kf])
            k.op('pool', lambda g: g.tensor_tensor(out=Rf[:], in0=Pr[:], in1=E1[:], op=ALU.mult), reads=[Pr, E1], writes=[Rf])
            k.op('dve', lambda v: v.scalar_tensor_tensor(out=rkm[:], in0=Pr[:], scalar=chp[:, 4, hp:hp + 1], in1=km[:],
                                                         op0=ALU.mult, op1=ALU.mult), reads=[Pr, chp, km], writes=[rkm])
            k.act(Vb[:], Pv[:], AF.Copy, reads=[Pv], writes=[Vb])
            tokb = R['tokb']; vtok = R['vtok']
            for half in range(2):
                pb = pbig[half]
                for cc in range(4):
                    cch = half * 4 + cc
                    csl = slice(cch * 64, (cch + 1) * 64)
                    for q, Xf in enumerate((Af, Bf, Kf, Vb)):
                        for e in range(2):
                            es_ = slice(e * 64, (e + 1) * 64)
                            k.mm(pb[es_, (cc * 4 + q) * 64:(cc * 4 + q + 1) * 64], Xf[es_, csl], idb[es_, es_], True, True,
                                 reads=[Xf, idb], writes=[pb])
                for b2 in range(2):
                    c0 = half * 4 + b2 * 2
                    k.act(tokb[:, hp, c0:c0 + 2, :, :].rearrange("p c q l -> p (c q l)"), pb[:, b2 * 512:(b2 + 1) * 512], AF.Copy,
                          reads=[pb], writes=[(tokb, hp)])
                for b2 in range(2):
                    c0 = half * 4 + b2 * 2
                    k.op('dve', lambda v: v.tensor_copy(out=vtok[:, hp, c0:c0 + 2, :],
                                                        in_=pb[:, b2 * 512:(b2 + 1) * 512].rearrange("p (c q l) -> p c q l", q=4, l=64)[:, :, 3, :]),
                         reads=[pb], writes=[(vtok, hp)])
            prk = psm[3]
            for cch in range(8):
                for e in range(2):
                    es_ = slice(e * 64, (e + 1) * 64)
                    k.mm(prk[es_, cch:cch + 1], rkm[es_, cch * 64:(cch + 1) * 64], R['onesb'][es_, 0:1], True, True,
                         reads=[rkm, R['onesb']], writes=[prk])
            k.act(R['rk'][:, hp, :], prk[:, 0:8], AF.Copy, reads=[prk], writes=[(R['rk'], hp)])
            amU = R['amU']
            for c2 in range(4):
                pm = psm[c2 % 2]
                for ci in range(2):
                    cch = c2 * 2 + ci
                    csl = slice(cch * 64, (cch + 1) * 64)
                    for q, (Lf, Rg) in enumerate(((Bf, Af), (Kf, Af), (Bf, Rf), (Kf, Rf))):
                        for e in range(2):
                            es_ = slice(e * 64, (e + 1) * 64)
                            k.mm(pm[es_, (ci * 4 + q) * 64:(ci * 4 + q + 1) * 64], Lf[es_, csl], Rg[es_, csl], True, True,
                                 reads=[Lf, Rg], writes=[pm])
                k.op('dve', lambda v: v.tensor_tensor(out=amU[:, hp, c2 * 2:c2 * 2 + 2, :, :].rearrange("p c q l -> p (c q l)"),
                                                      in0=pm[:, :], in1=mU[:].rearrange("p c q l -> p (c q l)"), op=ALU.mult),
                     reads=[pm, mU], writes=[(amU, hp)])
        tokb = R['tokb']; amU = R['amU']
        for hpp in range(2):
            hps = (2 * hpp, 2 * hpp + 1)
            for i, hp in enumerate(hps):
                Af, Bf = R['Af'][hp], R['Bf'][hp]
                NM = R['NM'][i]
                pn = psm[i]
                for cch in range(8):
                    csl = slice(cch * 64, (cch + 1) * 64)
                    for e in range(2):
                        es_ = slice(e * 64, (e + 1) * 64)
                        k.mm(pn[es_, cch * 64:(cch + 1) * 64], Af[es_, csl], Bf[es_, csl], True, True, reads=[Af, Bf], writes=[pn])
                k.op('dve', lambda v: v.tensor_tensor(out=NM[:, :, 0, :], in0=pn[:, :].rearrange("p (c l) -> p c l", l=64), in1=mL[:], op=ALU.mult),
                     reads=[pn, mL], writes=[NM])
                k.op('pool', lambda g: g.tensor_copy(out=NM[:, :, 1, :], in_=amU[:, hp, :, 0, :]), reads=[(amU, hp), NM], writes=[NM])
                X = pbig[i]
                for hf in range(2):
                    k.mm(X[:, hf * 512:(hf + 1) * 512], R['zb'][:, 0:128], R['zb'][:, :], True, False, reads=[R['zb']], writes=[X])
                for cch in range(8):
                    csl = slice(cch * 64, (cch + 1) * 64)
                    for e in range(2):
                        es_ = slice(e * 64, (e + 1) * 64)
                        k.mm(X[es_, cch * 128:cch * 128 + 64], Af[es_, csl], idb[es_, es_], False, False, reads=[Af, idb], writes=[X])
                        k.mm(X[es_, cch * 128 + 64:cch * 128 + 128], amU[es_, hp, cch, 1, :], tokb[es_, hp, cch, 3, :], False, False,
                             reads=[(amU, hp), (tokb, hp)], writes=[X])
            for st in range(6):
                for i, hp in enumerate(hps):
                    X = pbig[i]; Xb = R['Xb'][i]
                    for b2 in range(2):
                        xo = Xb[:, b2 * 4:(b2 + 1) * 4, :].rearrange("p c m -> p (c m)")
                        xi = X[:, b2 * 512:(b2 + 1) * 512]
                        if i == 0:
                            k.act(xo, xi, AF.Copy, reads=[X], writes=[Xb])
                        else:
                            k.op('dve', lambda v: v.tensor_copy(out=xo, in_=xi), reads=[X], writes=[Xb])
                for i, hp in enumerate(hps):
                    X = pbig[i]; Xb = R['Xb'][i]; NM = R['NM'][i]
                    for cch in range(8):
                        for e in range(2):
                            es_ = slice(e * 64, (e + 1) * 64)
                            k.mm(X[es_, cch * 128:(cch + 1) * 128], NM[es_, cch, 1, :], Xb[es_, cch, :], False, st == 5,
                                 reads=[NM, Xb], writes=[X])
                if st < 5:
                    for i, hp in enumerate(hps):
                        NM = R['NM'][i]
                        for hf in range(2):
                            pq = psm[i * 2 + hf]
                            for cc in range(4):
                                cch = hf * 4 + cc
                                for e in range(2):
                                    es_ = slice(e * 64, (e + 1) * 64)
                                    k.mm(pq[es_, (cc * 2) * 64:(cc * 2 + 1) * 64], NM[es_, cch, 1, :], NM[es_, cch, 0, :], True, True,
                                         reads=[NM], writes=[pq])
                                    k.mm(pq[es_, (cc * 2 + 1) * 64:(cc * 2 + 2) * 64], NM[es_, cch, 0, :], NM[es_, cch, 1, :], True, True,
                                         reads=[NM], writes=[pq])
                    for i, hp in enumerate(hps):
                        NM = R['NM'][i]
                        for hf in range(2):
                            pq = psm[i * 2 + hf]
                            dst = NM[:, hf * 4:(hf + 1) * 4, :, :].rearrange("p c q l -> p (c q l)")
                            if hf == 0:
                                k.act(dst, pq[:, :], AF.Copy, reads=[pq], writes=[NM])
                            else:
                                k.op('dve', lambda v: v.tensor_copy(out=dst, in_=pq[:, :]), reads=[pq], writes=[NM])
            for i, hp in enumerate(hps):
                X = pbig[i]; Xb = R['Xb'][i]
                for b2 in range(2):
                    Xv = X[:, b2 * 512:(b2 + 1) * 512].rearrange("p (c m) -> p c m", m=128)
                    k.act(Xb[:, b2 * 4:(b2 + 1) * 4, 0:64], Xv[:, :, 0:64], AF.Copy, reads=[X], writes=[Xb])
                for b2 in range(2):
                    Xv = X[:, b2 * 512:(b2 + 1) * 512].rearrange("p (c m) -> p c m", m=128)
                    k.op('dve', lambda v: v.tensor_copy(out=R['TAV'][:, hp, b2 * 4:(b2 + 1) * 4, :], in_=Xv[:, :, 64:128]),
                         reads=[X], writes=[(R['TAV'], hp)])
                pt = psm[i]
                for cch in range(8):
                    for e in range(2):
                        es_ = slice(e * 64, (e + 1) * 64)
                        k.mm(pt[es_, cch * 64:(cch + 1) * 64], Xb[es_, cch, 0:64], idb[es_, es_], True, True, reads=[Xb, idb], writes=[pt])
                k.act(R['TAT'][:, hp, :, :].rearrange("p c l -> p (c l)"), pt[:, :], AF.Copy, reads=[pt], writes=[(R['TAT'], hp)])
        S = R['S']; St = R['St']; Sb = R['Sb']; Ub = R['Ub']; TAT = R['TAT']; TAV = R['TAV']; Ysb = R['Ysb']; cL = R['cL']
        allhp = lambda B: [(B, hp) for hp in range(4)]
        for cch in range(8):
            csl = slice(cch * 64, (cch + 1) * 64)
            pU = psm[0]; pY = psm[1]; pS = psm[2]
            for hp in range(4):
                for e in range(2):
                    es_ = slice(e * 64, (e + 1) * 64)
                    k.mm(pU[es_, hp * 64:(hp + 1) * 64], TAT[es_, hp, cch, :], Sb[es_, hp, :], True, True, reads=[(TAT, hp), Sb], writes=[pU])
            k.op('dve', lambda v: v.tensor_tensor(out=Ub[:], in0=pU[:, 0:256].rearrange("p (h v) -> p h v", v=64), in1=TAV[:, :, cch, :], op=ALU.add),
                 reads=[pU] + allhp(TAV), writes=[Ub])
            for hp in range(4):
                for e in range(2):
                    es_ = slice(e * 64, (e + 1) * 64)
                    o_ = pY[es_, hp * 64:(hp + 1) * 64]
                    k.mm(o_, R['Rf'][hp][es_, csl], Sb[es_, hp, :], True, False, reads=[R['Rf'][hp], Sb], writes=[pY])
                    k.mm(o_, amU[es_, hp, cch, 2, :], Ub[es_, hp, :], False, False, reads=[(amU, hp), Ub], writes=[pY])
                    k.mm(o_, amU[es_, hp, cch, 3, :], tokb[es_, hp, cch, 3, :], False, True, reads=[(amU, hp), (tokb, hp)], writes=[pY])
            for hp in range(4):
                for e in range(2):
                    es_ = slice(e * 64, (e + 1) * 64)
                    o_ = pS[es_, hp * 64:(hp + 1) * 64]
                    k.mm(o_, tokb[es_, hp, cch, 1, :], Ub[es_, hp, :], True, False, reads=[(tokb, hp), Ub], writes=[pS])
                    k.mm(o_, tokb[es_, hp, cch, 2, :], tokb[es_, hp, cch, 3, :], False, True, reads=[(tokb, hp)], writes=[pS])
            k.op('dve', lambda v: v.tensor_tensor(out=St[:], in0=pS[:, 0:256].rearrange("p (h v) -> p h v", v=64), in1=S[:], op=ALU.add),
                 reads=[pS, S], writes=[St])
            cLb = cL[:, :, cch:cch + 1].to_broadcast([128, 4, 64])
            k.op('dve', lambda v: v.tensor_tensor(out=Sb[:], in0=St[:], in1=cLb, op=ALU.mult), reads=[St] + allhp(cL), writes=[Sb])
            k.op('pool', lambda g: g.tensor_tensor(out=S[:], in0=St[:], in1=cLb, op=ALU.mult), reads=[St] + allhp(cL), writes=[S])
            k.act(Ysb[:, :, cch, :], pY[:, 0:256].rearrange("p (h v) -> p h v", v=64), AF.Copy, reads=[pY], writes=[(Ysb, cch)])
        gst = R['gst']; Yt = R['Yt']; gn = R['gn']; rk = R['rk']; vtok = R['vtok']; gtok = R['TAV']; ygb = R['ygb']
        Yrd = [(Ysb, cch) for cch in range(8)]
        Y3 = Ysb[:].rearrange("p h c v -> p (h c) v")
        Yt3 = Yt[:].rearrange("p h c v -> p (h c) v")
        k.op('dve', lambda v: v.tensor_reduce(out=gst[:, 0, :], in_=Y3, axis=AX.X, op=ALU.add), reads=Yrd, writes=[gst])
        k.act(Yt[:].rearrange("p h c v -> p (h c v)"), Ysb[:].rearrange("p h c v -> p (h c v)"), AF.Square, reads=Yrd, writes=[Yt])
        k.op('dve', lambda v: v.tensor_reduce(out=gst[:, 1, :], in_=Yt3, axis=AX.X, op=ALU.add), reads=[Yt, gst], writes=[gst])
        k.op('dve', lambda v: v.tensor_scalar_mul(out=gst[:, 2, :], in0=gst[:, 0, :], scalar1=1.0 / 64.0), reads=[gst], writes=[gst])
        k.op('dve', lambda v: v.tensor_tensor(out=gst[:, 3, :], in0=gst[:, 2, :], in1=gst[:, 2, :], op=ALU.mult), reads=[gst], writes=[gst])
        k.op('dve', lambda v: v.scalar_tensor_tensor(out=gst[:, 3, :], in0=gst[:, 1, :], scalar=1.0 / 64.0, in1=gst[:, 3, :],
                                                     op0=ALU.mult, op1=ALU.subtract), reads=[gst], writes=[gst])
        k.op('dve', lambda v: v.tensor_scalar_add(out=gst[:, 3, :], in0=gst[:, 3, :], scalar1=64e-5), reads=[gst], writes=[gst])
        k.op('pool', lambda g: g.tensor_tensor(out=gst[:, 4, :], in0=gst[:, 3, :], in1=c['nhalf'][:, 0:32], op=ALU.pow), reads=[gst, c['nhalf']], writes=[gst])
        k.op('pool', lambda g: g.tensor_tensor(out=Yt3, in0=Y3, in1=gst[:, 2, :].unsqueeze(2).to_broadcast([128, 32, 64]), op=ALU.subtract),
             reads=Yrd + [gst, Yt], writes=[Yt])
        k.op('pool', lambda g: g.tensor_tensor(out=Yt3, in0=Yt3, in1=gst[:, 4, :].unsqueeze(2).to_broadcast([128, 32, 64]), op=ALU.mult),
             reads=[gst, Yt], writes=[Yt])
        k.op('pool', lambda g: g.tensor_tensor(out=Yt[:], in0=Yt[:], in1=gn[:, 0, :, :].unsqueeze(2).to_broadcast([128, 4, 8, 64]), op=ALU.mult),
             reads=[Yt, gn], writes=[Yt])
        k.op('pool', lambda g: g.tensor_tensor(out=Yt[:], in0=Yt[:], in1=gn[:, 1, :, :].unsqueeze(2).to_broadcast([128, 4, 8, 64]), op=ALU.add),
             reads=[Yt, gn], writes=[Yt])
        k.op('pool', lambda g: g.tensor_tensor(out=gtok[:], in0=vtok[:], in1=rk[:].unsqueeze(3).to_broadcast([128, 4, 8, 64]), op=ALU.mult),
             reads=allhp(vtok) + allhp(rk), writes=allhp(TAV))
        k.op('pool', lambda g: g.tensor_tensor(out=Yt[:], in0=Yt[:], in1=gtok[:], op=ALU.add), reads=[Yt] + allhp(TAV), writes=[Yt])
        g2 = R['g2']
        for hp in range(4):
            pg = psm[hp % 2]
            for cch in range(8):
                for e in range(2):
                    es_ = slice(e * 64, (e + 1) * 64)
                    k.mm(pg[es_, cch * 64:(cch + 1) * 64], sgb[:, cch * 64:(cch + 1) * 64], g2[:, hp * 128 + e * 64:hp * 128 + (e + 1) * 64],
                         True, True, reads=[sgb, g2], writes=[pg])
            k.op('dve', lambda v: v.tensor_tensor(out=ygb[:, hp, :, :], in0=pg[:, :].rearrange("p (c v) -> p c v", v=64), in1=Yt[:, hp, :, :], op=ALU.mult),
                 reads=[pg, Yt], writes=[(ygb, hp)])
        ygT = R['ygT']
        for hp in range(4):
            pt = psm[2 + hp % 2]
            for cch in range(8):
                for e in range(2):
                    es_ = slice(e * 64, (e + 1) * 64)
                    k.mm(pt[es_, cch * 64:(cch + 1) * 64], ygb[es_, hp, cch, :], idb[es_, es_], True, True, reads=[(ygb, hp), idb], writes=[pt])
            k.act(ygT[:, hp, :], pt[:, :], AF.Copy, reads=[pt], writes=[ygT])
        k.dma('sp', mix_out(blk), ygT[:], reads=[ygT], writes=['mixo'])


def mlstm_layout(inp, hh):
    W = inp['mlstm_w_in'][0]
    hs = [2 * hh, 2 * hh + 1]
    cols = []
    for h in hs: cols += list(range(h * 128, (h + 1) * 128))
    for h in hs: cols += list(range(512 + h * 128, 512 + (h + 1) * 128))
    for h in hs: cols += list(range(1024 + h * 256, 1024 + (h + 1) * 256))
    for h in hs: cols += list(range(2048 + h * 256, 2048 + (h + 1) * 256))
    cols += [3072 + hs[0], 3072 + hs[1], 3076 + hs[0], 3076 + hs[1]]
    cols += [3072] * 4
    Wsel = W[:, cols]
    win = np.ascontiguousarray(Wsel.reshape(8, 128, 1544).transpose(1, 0, 2))
    gb = np.tile(np.concatenate([inp['mlstm_b_i'][0][hs], inp['mlstm_b_f'][0][hs]])[None, :], (128, 1)).astype(np.float32)
    qkcols = cols[:512]
    cw = np.ascontiguousarray(inp['mlstm_conv_w'][0][:, qkcols].reshape(4, 4, 128).transpose(2, 1, 0))
    cb = np.ascontiguousarray(inp['mlstm_conv_b'][0][qkcols].reshape(4, 128).T)
    ng = np.tile(inp['mlstm_norm_g'][0][hs[0] * 256:(hs[1] + 1) * 256][None, :], (128, 1)).astype(np.float32)
    return dict(m_win=win, m_gb=gb, m_cw=cw, m_cb=cb, m_ng=ng)

def rwkv_layout(inp, hh):
    W = inp['rwkv_w_in'][0]
    d = 1024
    ch = list(range(hh * 512, (hh + 1) * 512))
    cols = ch + [d + c_ for c_ in ch] + [2 * d + c_ for c_ in ch] + list(range(3 * d, 3 * d + 256))
    Wsel = W[:, cols]
    win = np.ascontiguousarray(Wsel.reshape(8, 128, 1792).transpose(1, 0, 2))
    mu = np.ascontiguousarray(inp['rwkv_mu'][0][cols].reshape(14, 128).T)
    def colhp(v):
        return v[ch].reshape(4, 128).T
    chp = np.stack([colhp(inp['rwkv_w0'][0]), colhp(inp['rwkv_a0'][0]), colhp(inp['rwkv_k_k'][0]), colhp(inp['rwkv_k_a'][0]),
                    colhp(inp['rwkv_r_k'][0].reshape(-1))], axis=1).astype(np.float32)
    w2a2 = np.concatenate([inp['rwkv_w2'][0][:, ch], inp['rwkv_a2'][0][:, ch]], 0).astype(np.float32)
    g2 = np.ascontiguousarray(inp['rwkv_g2'][0][:, ch])
    gg = inp['rwkv_gn_g'][0][ch].reshape(4, 2, 64); gb = inp['rwkv_gn_b'][0][ch].reshape(4, 2, 64)
    gn = np.zeros((128, 2, 4, 64), np.float32)
    for e in range(2):
        gn[e * 64:(e + 1) * 64, 0] = gg[:, e, :][None]
        gn[e * 64:(e + 1) * 64, 1] = gb[:, e, :][None]
    return dict(r_win_d=win, r_mu_d=mu, r_chp_d=np.ascontiguousarray(chp), r_w2a2_d=w2a2, r_g2_d=g2, r_gn_d=gn)


def build_program(nb1=8, nb2=4, nb3=8, nb4=4):
    nc = bass.Bass("TRN2", target_bir_lowering=False)

    def din(name, shape, dt=F32):
        return nc.dram_tensor(name, list(shape), dt, kind="ExternalInput").ap()
    xT_d = din("xT", [1024, 4096])
    xres_d = din("xres", [1024, 2048])
    flags_d = din("flags", [128, 2])
    m_win = din("m_win", [128, 8, 1544]); m_gb = din("m_gb", [128, 4]); m_cw = din("m_cw", [128, 4, 4])
    m_cb = din("m_cb", [128, 4]); m_ng = din("m_ng", [128, 512])
    r_win = din("r_win_d", [128, 8, 1792]); r_mu = din("r_mu_d", [128, 14]); r_chp = din("r_chp_d", [128, 5, 4])
    r_w2a2 = din("r_w2a2_d", [128, 512]); r_g2 = din("r_g2_d", [128, 512]); r_gn = din("r_gn_d", [128, 2, 4, 64])
    wo_d = [din("wo%d" % l, [128, 8, 1024]) for l in range(2)]
    w1_d = [din("w1_%d" % l, [8, 128, 8, 512]) for l in range(2)]
    w2_d = [din("w2_%d" % l, [8, 128, 32, 128]) for l in range(2)]
    lnc_d = din("lnc", [128, 8, 8])
    outT = nc.dram_tensor("outT", [1024, 2048], F32, kind="ExternalOutput").ap()
    w1b = [nc.dram_tensor("w1b%d" % l, [8, 128, 8, 512], BF16).ap() for l in range(2)]
    w2b = [nc.dram_tensor("w2b%d" % l, [8, 128, 32, 128], BF16).ap() for l in range(2)]
    mix_in = [[nc.dram_tensor("mix_in%d_%d" % (l, j), [512, 2048], BF16) for j in range(2)] for l in range(2)]
    mix_g = [[nc.dram_tensor("mix_g%d_%d" % (l, j), [1024, 2048], BF16) for j in range(2)] for l in range(2)]
    x1f = nc.dram_tensor("x1f", [1024, 2048], F32).ap()
    x1b_in = [nc.dram_tensor("x1b_in%d" % j, [512, 2048], BF16) for j in range(2)]
    x1b_g = [nc.dram_tensor("x1b_g%d" % j, [1024, 2048], BF16) for j in range(2)]

    def fm(ap2d, t0, n=512):
        return ap2d[:, t0:t0 + n].rearrange("(c p) t -> p c t", p=128)

    with ExitStack() as es:
        k = KB(nc, es, n_dma_sems=40)
        ccs = [es.enter_context(nc.semaphore("cc%d" % i)) for i in range(14)]
        dmy_in = [nc.dram_tensor("dmy_in%d" % i, [128, 512], F32) for i in range(8)]
        dmy_out = [nc.dram_tensor("dmy_out%d" % i, [256, 512], F32) for i in range(8)]
        c = consts(k)
        PS = dict(pbig=[k.ps([128, 1024]) for _ in range(2)], psm=[k.ps([128, 512]) for _ in range(4)])
        lsb = k.sb([128, 8, 8], F32, "lsb")
        k.dma('sp', lsb[:], lnc_d[:, :, :], writes=[lsb])

        def gather(i, src_t, dst_t, inkey, outkey):
            k._wait('pool', k._deps('pool', [inkey], [outkey]))
            nc.gpsimd.collective_compute("AllGather", ALU.bypass, replica_groups=PAIRS,
                                         ins=[src_t.ap().opt()], outs=[dst_t.ap().opt()]).then_inc(ccs[i])
            nc.gpsimd.wait_ge(ccs[i], 1)
            k.op('pool', lambda g: g.memset(c['nhalf'][:, 63:64], -0.5), reads=[inkey], writes=[outkey])

        def convert_weights(l):
            for i in range(8):
                k.dma('pool', w1b[l][i].rearrange("p a b -> p (a b)"), w1_d[l][i].rearrange("p a b -> p (a b)"),
                      writes=[('w1b', i)], max_dma_last_dim=4096)
                k.dma('pool', w2b[l][i].rearrange("p a b -> p (a b)"), w2_d[l][i].rearrange("p a b -> p (a b)"),
                      writes=[('w2b', i)], max_dma_last_dim=4096)

        with k.scope():
            M = alloc_mlstm(k)
            mia = [t_.ap() for t_ in mix_in[0]]
            mlstm_phase(k, c, M, PS, xT_d, m_win, m_gb[:, :], m_cw[:, :, :], m_cb[:, :], m_ng[:, :],
                        mix_out=lambda blk: fm(mia[blk // 4], (blk % 4) * 512), nblk=nb1)
            convert_weights(0)
            k.barrier()
        k.es = es
        for j in range(2):
            gather(j, mix_in[0][j], mix_g[0][j], 'mixo', 'mixg')
        mg = [t_.ap() for t_ in mix_g[0]]
        xbi = [t_.ap() for t_ in x1b_in]
        for bb in range(nb2):
            with k.scope():
                P = alloc_mlp(k, c)
                P['pbig'] = PS['pbig']; P['psm'] = PS['psm']
                k.dma('sp', P['flags'][:], flags_d[:, :], writes=[P['flags']])
                mlp_phase(k, c, P, mixg=lambda t0: (fm(mg[0], t0), fm(mg[1], t0)), res_dram=lambda t0: fm(xres_d, t0),
                          wo_dram=wo_d[0], w1b=[w1b[0][i] for i in range(8)], w2b=[w2b[0][i] for i in range(8)],
                          lncols=(lsb, 0), out_f32_dram=lambda t0: fm(x1f, t0),
                          out_bf_dram=lambda t0: [fm(xbi[0], t0), fm(xbi[1], t0)], th=None, nblk=1, blk0=bb)
                k.barrier()
            k.es = es
            k.dma('sp', dmy_in[bb].ap()[:, :], flags_d[:, 0:1].to_broadcast([128, 512]) if False else xres_d[0:128, 0:512], writes=['dmyi'])
            gather(6 + bb, dmy_in[bb], dmy_out[bb], 'dmyi', 'dmyo')
        k.es = es
        for j in range(2):
            gather(2 + j, x1b_in[j], x1b_g[j], 'outb', 'xsrc')
        with k.scope():
            R = alloc_rwkv(k)
            xg = [t_.ap() for t_ in x1b_g]
            mib = [t_.ap() for t_ in mix_in[1]]

            def x_src(blk):
                half, off = divmod(blk * 512, 2048)
                return [xg[j][half * 512:(half + 1) * 512, off:off + 512].rearrange("(c p) t -> p c t", p=128) for j in range(2)]
            rwkv_phase(k, c, R, PS, x_src=x_src, win_d=r_win, mu_d=r_mu[:, :], chp_d=r_chp[:, :, :], w2a2_d=r_w2a2[:, :],
                       g2_d=r_g2[:, :], gn_d=r_gn[:, :, :, :], mix_out=lambda blk: fm(mib[blk // 4], (blk % 4) * 512), nblk=nb3)
            convert_weights(1)
            k.barrier()
        k.es = es
        for j in range(2):
            gather(4 + j, mix_in[1][j], mix_g[1][j], 'mixo', 'mixg')
        mg2 = [t_.ap() for t_ in mix_g[1]]
        for bb in range(nb4):
            with k.scope():
                P = alloc_mlp(k, c)
                P['pbig'] = PS['pbig']; P['psm'] = PS['psm']
                k.dma('sp', P['flags'][:], flags_d[:, :], writes=[P['flags']])
                mlp_phase(k, c, P, mixg=lambda t0: (fm(mg2[0], t0), fm(mg2[1], t0)), res_dram=lambda t0: fm(x1f, t0),
                          wo_dram=wo_d[1], w1b=[w1b[1][i] for i in range(8)], w2b=[w2b[1][i] for i in range(8)],
                          lncols=(lsb, 4), out_f32_dram=lambda t0: fm(outT, t0), out_bf_dram=None, th=None, nblk=1, blk0=bb)
                k.barrier()
            k.es = es
            if bb < nb4 - 1:
                k.dma('sp', dmy_in[4 + bb].ap()[:, :], xres_d[0:128, 0:512], writes=['dmyi'])
                gather(10 + bb, dmy_in[4 + bb], dmy_out[4 + bb], 'dmyi', 'dmyo')
        k.es = es
        k.finish()
    return nc


def _mlp_layout(inp, l, mix_key):
    wo = inp[mix_key][0]
    w1 = inp['mlp_w1'][l]
    w2 = inp['mlp_w2'][l]
    return (np.ascontiguousarray(wo.reshape(8, 128, 1024).transpose(1, 0, 2)),
            np.ascontiguousarray(w1.reshape(8, 128, 8, 512).transpose(2, 1, 0, 3)),
            np.ascontiguousarray(w2.reshape(32, 128, 8, 128).transpose(2, 1, 0, 3)))


_NC_CACHE = {}
_NBS = {}


def kernel(**inputs):
    inp = {k_: np.asarray(v, dtype=np.float32) for k_, v in inputs.items()}
    if 'nc' not in _NC_CACHE:
        _NC_CACHE['nc'] = build_program(**_NBS)
    nc = _NC_CACHE['nc']
    x = inp['x']
    ln = np.stack([inp['ln_mix_g'][0], inp['ln_mix_b'][0], inp['ln_ffn_g'][0], inp['ln_ffn_b'][0],
                   inp['ln_mix_g'][1], inp['ln_mix_b'][1], inp['ln_ffn_g'][1], inp['ln_ffn_b'][1]])
    lnc = np.ascontiguousarray(ln.reshape(8, 8, 128).transpose(2, 0, 1))
    wl = [_mlp_layout(inp, 0, 'mlstm_w_out'), _mlp_layout(inp, 1, 'rwkv_w_out')]
    lay_m = [mlstm_layout(inp, hh) for hh in range(2)]
    lay_r = [rwkv_layout(inp, hh) for hh in range(2)]
    in_maps = []
    for core in range(8):
        b, hh = divmod(core, 2)
        xT = np.ascontiguousarray(x[b].T)
        fl = np.zeros((128, 2), np.float32)
        fl[:, hh] = 1.0
        m = dict(xT=xT, xres=np.ascontiguousarray(xT[:, hh * 2048:(hh + 1) * 2048]), flags=fl, lnc=lnc,
                 wo0=wl[0][0], w1_0=wl[0][1], w2_0=wl[0][2], wo1=wl[1][0], w1_1=wl[1][1], w2_1=wl[1][2])
        m.update(lay_m[hh])
        m.update(lay_r[hh])
        in_maps.append(m)
    res = run_bass_kernel_spmd(nc, in_maps, core_ids=list(range(8)))
    out = np.empty((4, 4096, 1024), np.float32)
    for core in range(8):
        b, hh = divmod(core, 2)
        out[b, hh * 2048:(hh + 1) * 2048, :] = res.results[core]["outT"].T
    return out
```
